# Optimizing a Trainium2 kernel written in Bass

```python
import jax, jax.numpy as jnp
from jax import lax
import numpy as np

D_MODEL = 2048
BATCH = 2
SEQ = 4096
DEPTH = 2

HEAD_DIM = 64
MIX_WIDTH = D_MODEL
N_HEADS_TOTAL = MIX_WIDTH // HEAD_DIM
SB_HEADS = N_HEADS_TOTAL // 2
SWA_HEADS = N_HEADS_TOTAL - SB_HEADS
SWA_KV_HEADS = 4
SWA_GROUP = SWA_HEADS // SWA_KV_HEADS
WINDOW = 128
BLOCK = 128
D_FF = -(-8 * D_MODEL // (3 * 256)) * 256
RMS_EPS = 1e-5

SB_W = SB_HEADS * HEAD_DIM
SWA_Q_W = SWA_HEADS * HEAD_DIM
SWA_KV_W = SWA_KV_HEADS * HEAD_DIM
IN_COLS = 3 * SB_W + SWA_Q_W + 2 * SWA_KV_W
SPLITS = (SB_W, 2 * SB_W, 3 * SB_W, 3 * SB_W + SWA_Q_W, 3 * SB_W + SWA_Q_W + SWA_KV_W)

kernel_name = "hymba_stickbreak_swa_sink_alibi_swiglu"


def rmsnorm(x, g):
    xf = x.astype(jnp.float32)
    y = xf * lax.rsqrt(jnp.mean(xf * xf, axis=-1, keepdims=True) + RMS_EPS)
    return (y * g.astype(jnp.float32)).astype(x.dtype)


def stick_breaking_attention(q, k, v):
    B, S, H, Dh = q.shape
    qf = q.astype(jnp.float32) * (Dh ** -0.5)
    kf = k.astype(jnp.float32)
    vf = v.astype(jnp.float32)
    outs = []
    for i in range(S // BLOCK):
        q0 = i * BLOCK
        kend = q0 + BLOCK
        z = jnp.einsum('bqhd,bkhd->bhqk', qf[:, q0:kend], kf[:, :kend])
        t = q0 + jnp.arange(BLOCK)[:, None]
        s = jnp.arange(kend)[None, :]
        causal = s < t
        log_beta = jax.nn.log_sigmoid(z)
        log_one_minus = jnp.where(causal, jax.nn.log_sigmoid(-z), 0.0)
        survive = lax.cumsum(log_one_minus, axis=3, reverse=True) - log_one_minus
        weights = jnp.where(causal, jnp.exp(log_beta + survive), 0.0)
        outs.append(jnp.einsum('bhqk,bkhd->bqhd', weights, vf[:, :kend]))
    return jnp.concatenate(outs, axis=1).astype(q.dtype)


def sliding_window_sink_attention(q, k, v, sinks, slopes):
    B, S, H, Dh = q.shape
    G = k.shape[2]
    R = H // G
    nb = S // BLOCK
    f32 = jnp.float32
    qb = q.astype(f32).reshape(B, nb, BLOCK, G, R, Dh) * (Dh ** -0.5)
    pad = jnp.zeros((B, BLOCK, G, Dh), f32)
    kp = jnp.concatenate([pad, k.astype(f32)], axis=1).reshape(B, nb + 1, BLOCK, G, Dh)
    vp = jnp.concatenate([pad, v.astype(f32)], axis=1).reshape(B, nb + 1, BLOCK, G, Dh)
    kband = jnp.concatenate([kp[:, :-1], kp[:, 1:]], axis=2)
    vband = jnp.concatenate([vp[:, :-1], vp[:, 1:]], axis=2)
    scores = jnp.einsum('bnqgrd,bnkgd->bngrqk', qb, kband)
    qpos = jnp.arange(BLOCK)[:, None] + BLOCK
    kpos = jnp.arange(2 * BLOCK)[None, :]
    dist = (qpos - kpos).astype(f32)
    in_window = (dist >= 0) & (dist < WINDOW)
    blk = jnp.arange(nb)[:, None, None]
    real_key = (blk * BLOCK + kpos[None] - BLOCK) >= 0
    mask = in_window[None] & real_key
    m_h = slopes.astype(f32).reshape(G, R)[:, :, None, None]
    scores = scores - m_h * dist
    scores = jnp.where(mask[None, :, None, None], scores, -jnp.inf)
    sink = sinks.astype(f32).reshape(1, 1, G, R, 1, 1)
    mx = jnp.maximum(jnp.max(scores, axis=-1, keepdims=True), sink)
    p = jnp.exp(scores - mx)
    denom = jnp.sum(p, axis=-1, keepdims=True) + jnp.exp(sink - mx)
    probs = p / denom
    out = jnp.einsum('bngrqk,bnkgd->bnqgrd', probs, vband)
    return out.reshape(B, S, H, Dh).astype(q.dtype)


def setup_inputs(seed: int = 0) -> dict:
    key = jax.random.key(seed)
    ks = jax.random.split(key, 12)
    f32 = jnp.float32
    x = jax.random.normal(ks[0], (BATCH, SEQ, D_MODEL), f32)
    ln_mix = 1.0 + 0.02 * jax.random.normal(ks[1], (DEPTH, D_MODEL), f32)
    w_in = jax.random.normal(ks[2], (DEPTH, D_MODEL, IN_COLS), f32) * D_MODEL ** -0.5
    sb_out_norm = 1.0 + 0.02 * jax.random.normal(ks[3], (DEPTH, SB_W), f32)
    swa_out_norm = 1.0 + 0.02 * jax.random.normal(ks[4], (DEPTH, SWA_Q_W), f32)
    swa_sinks = 0.5 * jax.random.normal(ks[5], (DEPTH, SWA_HEADS), f32)
    w_out = jax.random.normal(ks[6], (DEPTH, MIX_WIDTH, D_MODEL), f32) * (MIX_WIDTH * 2 * DEPTH) ** -0.5
    ln_ffn = 1.0 + 0.02 * jax.random.normal(ks[7], (DEPTH, D_MODEL), f32)
    w_gate_up = jax.random.normal(ks[8], (DEPTH, D_MODEL, 2 * D_FF), f32) * D_MODEL ** -0.5
    w_down = jax.random.normal(ks[9], (DEPTH, D_FF, D_MODEL), f32) * (D_FF * 2 * DEPTH) ** -0.5
    ln_final = 1.0 + 0.02 * jax.random.normal(ks[10], (D_MODEL,), f32)
    return {"x": x, "ln_mix": ln_mix, "w_in": w_in, "sb_out_norm": sb_out_norm,
            "swa_out_norm": swa_out_norm, "swa_sinks": swa_sinks, "w_out": w_out,
            "ln_ffn": ln_ffn, "w_gate_up": w_gate_up, "w_down": w_down, "ln_final": ln_final}


def reference(x, ln_mix, w_in, sb_out_norm, swa_out_norm, swa_sinks, w_out,
              ln_ffn, w_gate_up, w_down, ln_final):
    B, S, _ = x.shape
    slopes = 2.0 ** (-8.0 * (jnp.arange(SWA_HEADS, dtype=jnp.float32) + 1.0) / SWA_HEADS)
    for l in range(DEPTH):
        h = rmsnorm(x, ln_mix[l])
        proj = h @ w_in[l]
        sb_q, sb_k, sb_v, sw_q, sw_k, sw_v = jnp.split(proj, SPLITS, axis=-1)
        sb_o = stick_breaking_attention(
            sb_q.reshape(B, S, SB_HEADS, HEAD_DIM),
            sb_k.reshape(B, S, SB_HEADS, HEAD_DIM),
            sb_v.reshape(B, S, SB_HEADS, HEAD_DIM)).reshape(B, S, SB_W)
        sw_o = sliding_window_sink_attention(
            sw_q.reshape(B, S, SWA_HEADS, HEAD_DIM),
            sw_k.reshape(B, S, SWA_KV_HEADS, HEAD_DIM),
            sw_v.reshape(B, S, SWA_KV_HEADS, HEAD_DIM),
            swa_sinks[l], slopes).reshape(B, S, SWA_Q_W)
        mixed = jnp.concatenate([rmsnorm(sb_o, sb_out_norm[l]),
                                 rmsnorm(sw_o, swa_out_norm[l])], axis=-1)
        x = x + mixed @ w_out[l]
        h = rmsnorm(x, ln_ffn[l])
        gate, up = jnp.split(h @ w_gate_up[l], 2, axis=-1)
        x = x + (jax.nn.silu(gate) * up) @ w_down[l]
    return rmsnorm(x, ln_final)
```

```python
import numpy as np
import ml_dtypes
import concourse.bass as bass
import concourse.mybir as mybir
from concourse.bass_utils import run_bass_kernel_spmd

F32 = mybir.dt.float32
BF16 = mybir.dt.bfloat16
AF = mybir.ActivationFunctionType
ALU = mybir.AluOpType
NPBF = ml_dtypes.bfloat16

P = 128
D = 2048
KC = D // P
S_LEN = 4096
T_OWN = 1024
DFF = 5632
NFF = DFF // P
HD = 64
EPS = 1e-5
NEG = -30000.0
NCORES = 8
C_QSB, C_KSB, C_QSW, C_KSW, C_VSB, C_VSW, C_END = 0, 256, 512, 768, 896, 1152, 1216

ENGS = ("pe", "act", "dve", "pool", "sp")


class Tok:
    __slots__ = ("kind", "eng", "sem", "val")

    def __init__(self, kind, eng):
        self.kind = kind
        self.eng = eng
        self.sem = None
        self.val = None


class Sched:
    def __init__(self, nc, same_engine_sync=True):
        self.nc = nc
        self.ops = {e: [] for e in ENGS}
        self.same_engine_sync = same_engine_sync
        self.pending_barrier = {e: [] for e in ENGS}
        self.last = {e: None for e in ENGS}

    def _add(self, kind, eng, fn, deps, slot):
        tok = Tok(kind, eng)
        dl = [d for d in deps if d is not None]
        if self.pending_barrier[eng]:
            dl += self.pending_barrier[eng]
            self.pending_barrier[eng] = []
        self.ops[eng].append(dict(fn=fn, deps=dl, tok=tok, slot=slot))
        self.last[eng] = tok
        return tok

    def op(self, eng, fn, deps=()):
        return self._add("op", eng, fn, deps, None)

    def dma(self, eng, fn, slot, deps=()):
        return self._add("dma", eng, fn, deps, slot)

    def barrier(self, extra=()):
        toks = [t for t in self.last.values() if t is not None] + [t for t in extra if t is not None]
        for e in ENGS:
            self.pending_barrier[e] = list(toks)

    def finalize(self):
        nc = self.nc
        used = set()
        for e in ENGS:
            for o in self.ops[e]:
                for d in o["deps"]:
                    if d.kind == "op" and d.eng == e and not self.same_engine_sync:
                        continue
                    used.add(id(d))
        self.eng_sem = {e: nc.alloc_semaphore(f"s_{e}") for e in ENGS}
        slot_sem, slot_cnt = {}, {}
        for e in ENGS:
            cnt = 0
            for o in self.ops[e]:
                t = o["tok"]
                if t.kind == "op":
                    if id(t) in used:
                        cnt += 1
                        t.sem = self.eng_sem[e]
                        t.val = cnt
                else:
                    s = o["slot"]
                    if s not in slot_sem:
                        slot_sem[s] = nc.alloc_semaphore(f"d_{s}")
                        slot_cnt[s] = 0
                    slot_cnt[s] += 16
                    t.sem = slot_sem[s]
                    t.val = slot_cnt[s]

    def emit(self, e, eng):
        waited = {}
        for o in self.ops[e]:
            need = {}
            for d in o["deps"]:
                if d.kind == "op" and d.eng == e and not self.same_engine_sync:
                    continue
                assert d.sem is not None, "dep on unsignaled op"
                k = id(d.sem)
                if k not in need or need[k][1] < d.val:
                    need[k] = (d.sem, d.val)
            for k, (sem, val) in need.items():
                if waited.get(k, 0) >= val:
                    continue
                eng.wait_ge(sem, val)
                waited[k] = val
            if o["fn"] is None:
                continue
            ins = o["fn"](eng)
            t = o["tok"]
            if t.kind == "dma":
                ins.then_inc(t.sem, 16)
            elif t.sem is not None:
                ins.then_inc(t.sem, 1)

    def run_block(self, final_waits=()):
        nc = self.nc
        self.op("sp", None, deps=list(final_waits))
        self.finalize()
        with nc.Block() as block:
            @block.tensor
            def _(eng):
                self.emit("pe", eng)

            @block.scalar
            def _(eng):
                self.emit("act", eng)

            @block.vector
            def _(eng):
                self.emit("dve", eng)

            @block.gpsimd
            def _(eng):
                self.emit("pool", eng)

            @block.sync
            def _(eng):
                self.emit("sp", eng)


class Buf:
    def __init__(self):
        self.w = None
        self.r = []

    def wdeps(self):
        return list(self.r) + ([self.w] if self.w is not None else [])

    def wrote(self, tok):
        self.w = tok
        self.r = []

    def rdeps(self):
        return [self.w] if self.w is not None else []

    def read(self, tok):
        if tok.kind == "op":
            self.r = [t for t in self.r if not (t.kind == "op" and t.eng == tok.eng)]
        self.r.append(tok)


def _const_tables():
    p = np.arange(P)[:, None]
    ident = np.eye(P, dtype=np.float32)
    negtri = np.where(np.arange(P)[:, None] >= np.arange(P)[None, :], -1.0, 0.0).astype(np.float32)
    negones = -np.ones((P, P), np.float32)
    tl = np.arange(512)[None, :]
    maskb = np.stack([np.where(tl > P * i + p, 0.0, NEG) for i in range(4)]).astype(np.float32)
    c128 = np.concatenate([ident, negtri, negones], axis=1)
    return c128.astype(NPBF), np.ascontiguousarray(maskb.transpose(1, 0, 2)).astype(NPBF)


def _swa_tables(g):
    p = np.arange(P)[:, None].astype(np.float64)
    tl = np.arange(256)[None, :].astype(np.float64)
    dist = tl - p
    his, los = [], []
    for hq in range(4):
        h = 4 * g + hq
        slope = np.float32(2.0) ** np.float32(-8.0 * (h + 1.0) / 16.0)
        b = np.where((dist >= 0) & (dist < 128), -np.float64(slope) * dist, NEG).astype(np.float32)
        hi = b.astype(NPBF)
        lo = (b - hi.astype(np.float32)).astype(NPBF)
        his.append(hi)
        los.append(lo)
    return np.ascontiguousarray(np.stack(his + los, axis=1))


def phase_B(nc, S, HT_d, w_d, c128_d, maskb_d, swab_d, sink_d, oT_d, psum, hT_ready=None, dbg_v=None):
    A = nc.alloc_sbuf_tensor
    w_sb = A("b_w", [P, KC, C_END], BF16)
    hbuf = [A(f"b_h{i}", [P, KC, 512], BF16) for i in range(2)]
    qT = A("b_qT", [P, 2, S_LEN], BF16)
    kT = A("b_kT", [P, 2, S_LEN], BF16)
    qwT = A("b_qwT", [P, 2, S_LEN], BF16)
    kwT = A("b_kwT", [P, S_LEN], BF16)
    vall = A("b_vall", [P, 32, 320], BF16)
    v = vall[:, :, 0:256]
    vw = vall[:, :, 256:320]
    c128 = A("b_c128", [P, 3 * P], BF16)
    maskb = A("b_maskb", [P, 4, 512], BF16)
    swab = A("b_swab", [P, 8, 256], BF16)
    sink = A("b_sink", [64, 4], F32)
    esink = A("b_esink", [64, 4], F32)
    ones64 = A("b_ones64", [P, 64], BF16)
    e_t = [A(f"b_e{i}", [P, 512], F32) for i in range(2)]
    sp_t = [A(f"b_sp{i}", [P, 512], BF16) for i in range(2)]
    ssum = A("b_ssum", [P, 512], F32)
    ssb_t = [A(f"b_ssb{i}", [P, 512], BF16) for i in range(2)]
    a_t = [A(f"b_a{i}", [P, 512], BF16) for i in range(2)]
    ob_t = [A(f"b_ob{i}", [64, S_LEN], BF16) for i in range(2)]
    pt_t = [A(f"b_pt{i}", [P, 256], BF16) for i in range(2)]
    den_t = A("b_den", [64, 512], F32)
    ident = c128[:, 0:P]
    negtri = c128[:, P:2 * P]
    negones = c128[:, 2 * P:3 * P]

    ld_c = S.dma("sp", lambda e: e.dma_start(out=c128[:], in_=c128_d), "bc0")
    ld_m = S.dma("sp", lambda e: e.dma_start(out=maskb[:], in_=maskb_d), "bc1")
    ld_sb = S.dma("sp", lambda e: e.dma_start(out=swab[:], in_=swab_d), "bc2")
    ld_sk = S.dma("sp", lambda e: e.dma_start(out=sink[:], in_=sink_d), "bc3")
    m_one = S.op("dve", lambda e: e.memset(ones64[:], 1.0))
    ex_sk = S.op("act", lambda e: e.activation(out=esink[:], in_=sink[:], func=AF.Exp), deps=[ld_sk])
    w_ld = []
    wv = w_d.rearrange("(kc p) n -> p kc n", p=P)
    for kc in range(KC):
        w_ld.append(S.dma("pool", lambda e, kc=kc: e.dma_start(out=w_sb[:, kc, :], in_=wv[:, kc, :]), f"bw{kc}"))

    hB = [Buf(), Buf()]
    psB = [Buf() for _ in range(8)]
    pi = 0
    nev = 0
    fm_blocks = [
        (C_QSB, P, lambda t0: qT[:, 0, t0:t0 + 512], 0.125),
        (C_QSB + P, P, lambda t0: qT[:, 1, t0:t0 + 512], 0.125),
        (C_KSB, P, lambda t0: kT[:, 0, t0:t0 + 512], None),
        (C_KSB + P, P, lambda t0: kT[:, 1, t0:t0 + 512], None),
        (C_QSW, P, lambda t0: qwT[:, 0, t0:t0 + 512], 0.125),
        (C_QSW + P, P, lambda t0: qwT[:, 1, t0:t0 + 512], 0.125),
        (C_KSW, P, lambda t0: kwT[:, t0:t0 + 512], None),
    ]
    proj_done = []
    for tcn in range(8):
        b = tcn % 2
        r, off = tcn // 2, (tcn % 2) * 512
        t0 = tcn * 512
        src = HT_d[r, :, off:off + 512].rearrange("(kc p) t -> p kc t", p=P)
        ld_h = S.dma("sp", lambda e, b=b, src=src: e.dma_start(out=hbuf[b][:], in_=src), f"bh{b}",
                     deps=hB[b].wdeps() + ([hT_ready] if hT_ready is not None else []))
        hB[b].wrote(ld_h)
        for (c0, M, dst, scale) in fm_blocks:
            pb = pi % 4
            pi += 1
            for kc in range(KC):
                mm = S.op("pe", lambda e, pb=pb, kc=kc, c0=c0, M=M, b=b: e.matmul(
                    psum[pb][0:M, :], lhsT=w_sb[:, kc, c0:c0 + M], rhs=hbuf[b][:, kc, :], start=(kc == 0), stop=(kc == KC - 1)),
                    deps=(psB[pb].wdeps() + hB[b].rdeps() if kc == 0 else []) + [w_ld[kc]])
            psB[pb].wrote(mm)
            hB[b].read(mm)
            eng = "act" if nev % 2 == 0 else "dve"
            nev += 1
            if eng == "act":
                if scale is None:
                    ev = S.op("act", lambda e, pb=pb, M=M, dst=dst, t0=t0: e.activation(out=dst(t0), in_=psum[pb][0:M, :], func=AF.Copy),
                              deps=psB[pb].rdeps())
                else:
                    ev = S.op("act", lambda e, pb=pb, M=M, dst=dst, t0=t0, scale=scale: e.mul(out=dst(t0), in_=psum[pb][0:M, :], mul=scale),
                              deps=psB[pb].rdeps())
            else:
                if scale is None:
                    ev = S.op("dve", lambda e, pb=pb, M=M, dst=dst, t0=t0: e.tensor_copy(out=dst(t0), in_=psum[pb][0:M, :]),
                              deps=psB[pb].rdeps())
                else:
                    ev = S.op("dve", lambda e, pb=pb, M=M, dst=dst, t0=t0, scale=scale: e.tensor_scalar(
                        out=dst(t0), in0=psum[pb][0:M, :], scalar1=scale, scalar2=None, op0=ALU.mult), deps=psB[pb].rdeps())
            psB[pb].read(ev)
            proj_done.append(ev)
        for tb in range(4):
            blk = tcn * 4 + tb
            pb = pi % 4
            pi += 1
            for kc in range(KC):
                mm = S.op("pe", lambda e, pb=pb, kc=kc, b=b, tb=tb: e.matmul(
                    psum[pb][:, 0:320], lhsT=hbuf[b][:, kc, tb * P:(tb + 1) * P], rhs=w_sb[:, kc, C_VSB:C_END],
                    start=(kc == 0), stop=(kc == KC - 1)),
                    deps=(psB[pb].wdeps() + hB[b].rdeps() if kc == 0 else []) + [w_ld[kc]])
            psB[pb].wrote(mm)
            hB[b].read(mm)
            if blk % 2 == 0:
                ev1 = S.op("act", lambda e, pb=pb, blk=blk: e.activation(out=vall[:, blk, :], in_=psum[pb][:, 0:320], func=AF.Copy),
                           deps=psB[pb].rdeps())
            else:
                ev1 = S.op("dve", lambda e, pb=pb, blk=blk: e.tensor_copy(out=vall[:, blk, :], in_=psum[pb][:, 0:320]),
                           deps=psB[pb].rdeps())
            ev2 = ev1
            psB[pb].read(ev1)
            proj_done += [ev1, ev2]

    S.barrier(extra=[ld_c, ld_m, ld_sb, m_one, ex_sk] + proj_done[-16:])
    dbg = []
    if dbg_v is not None:
        dbg.append(S.dma("sp", lambda e: e.dma_start(out=dbg_v, in_=vall[:, :, 0:256]), "dbgv"))

    units = []
    for h in range(4):
        for qc in range(8):
            nkb = 4 * qc + 4
            for kb in reversed(range(nkb)):
                units.append(dict(h=h, qc=qc, kb=kb, first=(kb == nkb - 1), last=(kb == 0), diag=(kb >= 4 * qc), i=kb - 4 * qc,
                                  g=h * 8 + qc))
    NU = len(units)
    zB = [Buf(), Buf()]
    cB = [Buf(), Buf()]
    oB = [Buf(), Buf()]
    eB = [Buf(), Buf()]
    spB = [Buf(), Buf()]
    ssbB = [Buf(), Buf()]
    aB = [Buf(), Buf()]
    ssumB = Buf()
    obB = [Buf(), Buf()]
    out_dmas = []

    def kq(u):
        hp, pbs = u["h"] // 2, 64 * (u["h"] % 2)
        kb, qc = u["kb"], u["qc"]
        return (kT[pbs:pbs + 64, hp, kb * P:(kb + 1) * P], qT[pbs:pbs + 64, hp, qc * 512:(qc + 1) * 512])

    def pe_z(ui):
        u = units[ui]
        z = ui % 2
        kk, qq = kq(u)
        t = S.op("pe", lambda e: e.matmul(psum[z][:], lhsT=kk, rhs=qq, start=True, stop=not u["diag"]), deps=zB[z].wdeps())
        if u["diag"]:
            t = S.op("pe", lambda e: e.matmul(psum[z][:], lhsT=ident, rhs=maskb[:, u["i"], :], start=False, stop=True))
        zB[z].wrote(t)

    def act_z(ui):
        z = ui % 2
        t = S.op("act", lambda e: e.activation(out=e_t[z][:], in_=psum[z][:], func=AF.Exp), deps=zB[z].rdeps() + eB[z].wdeps())
        zB[z].read(t)
        eB[z].wrote(t)
        t2 = S.op("act", lambda e: e.activation(out=sp_t[z][:], in_=e_t[z][:], func=AF.Ln, bias=1.0),
                  deps=eB[z].rdeps() + spB[z].wdeps())
        eB[z].read(t2)
        spB[z].wrote(t2)

    def dve_ssum(ui):
        u = units[ui]
        z = ui % 2
        if u["last"]:
            return
        if u["first"]:
            t = S.op("dve", lambda e: e.tensor_copy(out=ssum[:], in_=sp_t[z][:]), deps=spB[z].rdeps() + ssumB.wdeps())
        else:
            t = S.op("dve", lambda e: e.tensor_tensor(out=ssum[:], in0=ssum[:], in1=sp_t[z][:], op=ALU.add),
                     deps=spB[z].rdeps() + ssumB.wdeps())
        spB[z].read(t)
        ssumB.wrote(t)
        t2 = S.op("dve", lambda e: e.tensor_copy(out=ssb_t[z][:], in_=ssum[:]), deps=ssumB.rdeps() + ssbB[z].wdeps())
        ssumB.read(t2)
        ssbB[z].wrote(t2)

    def pe_c(ui):
        u = units[ui]
        z = ui % 2
        kk, qq = kq(u)
        t = S.op("pe", lambda e: e.matmul(psum[2 + z][:], lhsT=kk, rhs=qq, start=True, stop=False), deps=cB[z].wdeps())
        more = []
        more.append((negtri, sp_t[z][:], spB[z].rdeps(), spB[z]))
        if not u["first"]:
            pz = (ui - 1) % 2
            more.append((negones, ssb_t[pz][:], ssbB[pz].rdeps(), ssbB[pz]))
        if u["diag"]:
            more.append((ident, maskb[:, u["i"], :], [], None))
        for j, (lh, rh, dp, bufr) in enumerate(more):
            t = S.op("pe", lambda e, lh=lh, rh=rh, j=j: e.matmul(psum[2 + z][:], lhsT=lh, rhs=rh, start=False, stop=(j == len(more) - 1)), deps=dp)
            if bufr is not None:
                bufr.read(t)
        cB[z].wrote(t)

    def act_c(ui):
        z = ui % 2
        t = S.op("act", lambda e: e.activation(out=a_t[z][:], in_=psum[2 + z][:], func=AF.Exp), deps=cB[z].rdeps() + aB[z].wdeps())
        cB[z].read(t)
        aB[z].wrote(t)

    def pe_o(ui):
        u = units[ui]
        z = ui % 2
        ob = u["g"] % 2
        h, kb = u["h"], u["kb"]
        t = S.op("pe", lambda e: e.matmul(psum[4 + ob][0:64, :], lhsT=v[:, kb, h * 64:(h + 1) * 64], rhs=a_t[z][:],
                                          start=u["first"], stop=u["last"]),
                 deps=aB[z].rdeps() + (oB[ob].wdeps() if u["first"] else []))
        aB[z].read(t)
        if u["last"]:
            oB[ob].wrote(t)
            hb = h % 2
            qc = u["qc"]
            ev = S.op("dve", lambda e: e.tensor_copy(out=ob_t[hb][:, qc * 512:(qc + 1) * 512], in_=psum[4 + ob][0:64, :]),
                      deps=oB[ob].rdeps() + (obB[hb].wdeps() if qc == 0 else []))
            oB[ob].read(ev)
            if qc == 7:
                obB[hb].wrote(ev)
                st = S.dma("sp", lambda e: e.dma_start(out=oT_d[h * 64:(h + 1) * 64, :], in_=ob_t[hb][:]), f"bo{hb}", deps=[ev])
                obB[hb].read(st)
                out_dmas.append(st)

    for r in range(-1, NU + 1):
        if 0 <= r + 1 < NU:
            pe_z(r + 1)
        if 0 <= r < NU:
            pe_c(r)
        if 0 <= r - 1 < NU:
            pe_o(r - 1)
        if 0 <= r + 1 < NU:
            act_z(r + 1)
            dve_ssum(r + 1)
        if 0 <= r < NU:
            act_c(r)

    S.barrier()

    sB = [Buf(), Buf()]
    nB = [Buf(), Buf()]
    dB = [Buf(), Buf()]
    ptB = [Buf(), Buf()]
    denB = Buf()
    wunits = [(hq, kb) for hq in range(4) for kb in range(32)]
    NW = len(wunits)

    def w_s(ui):
        hq, kb = wunits[ui]
        z = ui % 2
        hp, pbs = hq // 2, 64 * (hq % 2)
        N = 256 if kb < 31 else 128
        t = S.op("pe", lambda e: e.matmul(psum[z][:, 0:N], lhsT=kwT[pbs:pbs + 64, kb * P:(kb + 1) * P],
                                          rhs=qwT[pbs:pbs + 64, hp, kb * P:kb * P + N], start=True, stop=False), deps=sB[z].wdeps())
        S.op("pe", lambda e: e.matmul(psum[z][:, 0:N], lhsT=ident, rhs=swab[:, hq, 0:N], start=False, stop=False))
        t = S.op("pe", lambda e: e.matmul(psum[z][:, 0:N], lhsT=ident, rhs=swab[:, 4 + hq, 0:N], start=False, stop=True))
        sB[z].wrote(t)

    def w_exp(ui):
        hq, kb = wunits[ui]
        z = ui % 2
        N = 256 if kb < 31 else 128
        t = S.op("act", lambda e: e.activation(out=pt_t[z][:, 0:N], in_=psum[z][:, 0:N], func=AF.Exp), deps=sB[z].rdeps() + ptB[z].wdeps())
        sB[z].read(t)
        ptB[z].wrote(t)

    def w_pv(ui):
        hq, kb = wunits[ui]
        z = ui % 2
        h = hq
        for half in range(2):
            i = kb + half
            if i > 31:
                continue
            cb = (i // 4) % 2
            cs = slice((i % 4) * P, (i % 4 + 1) * P)
            start = (half == 1) or (kb == 0)
            stop = (half == 0)
            first_of_bank = (i % 4 == 0) and start
            t1 = S.op("pe", lambda e, cb=cb, cs=cs, half=half, start=start, stop=stop: e.matmul(
                psum[2 + cb][0:64, cs], lhsT=vw[:, kb, :], rhs=pt_t[z][:, half * P:(half + 1) * P], start=start, stop=stop),
                deps=ptB[z].rdeps() + (nB[cb].wdeps() if first_of_bank else []))
            t2 = S.op("pe", lambda e, cb=cb, cs=cs, half=half, start=start, stop=stop: e.matmul(
                psum[4 + cb][0:64, cs], lhsT=ones64[:], rhs=pt_t[z][:, half * P:(half + 1) * P], start=start, stop=stop),
                deps=(dB[cb].wdeps() if first_of_bank else []))
            ptB[z].read(t2)
            if stop and i % 4 == 3:
                nB[cb].wrote(t1)
                dB[cb].wrote(t2)
                c = i // 4
                hb = h % 2
                d1 = S.op("dve", lambda e, cb=cb: e.tensor_scalar(out=den_t[:], in0=psum[4 + cb][0:64, :], scalar1=esink[:, hq:hq + 1],
                                                                   scalar2=None, op0=ALU.add), deps=dB[cb].rdeps() + denB.wdeps())
                dB[cb].read(d1)
                d2 = S.op("dve", lambda e: e.reciprocal(out=den_t[:], in_=den_t[:]), deps=[d1])
                d3 = S.op("dve", lambda e, cb=cb, c=c, hb=hb: e.tensor_tensor(out=ob_t[hb][:, c * 512:(c + 1) * 512], in0=psum[2 + cb][0:64, :],
                                                                             in1=den_t[:], op=ALU.mult),
                          deps=[d2] + nB[cb].rdeps() + (obB[hb].wdeps() if c == 0 else []))
                nB[cb].read(d3)
                denB.wrote(d3)
                if c == 7:
                    obB[hb].wrote(d3)
                    st = S.dma("sp", lambda e, hb=hb, h=h: e.dma_start(out=oT_d[256 + h * 64:256 + (h + 1) * 64, :], in_=ob_t[hb][:]),
                               f"bo{hb}", deps=[d3])
                    obB[hb].read(st)
                    out_dmas.append(st)

    for r in range(-1, NW):
        if r + 1 < NW:
            w_s(r + 1)
        if r >= 0:
            w_pv(r)
        if r + 1 < NW:
            w_exp(r + 1)
    return out_dmas + dbg


def build_B(debug=False):
    nc = bass.Bass("TRN2", target_bir_lowering=False)
    HT_d = nc.dram_tensor("HT", [4, D, T_OWN], BF16, kind="ExternalInput").ap()
    w_d = nc.dram_tensor("w_in", [D, C_END], F32, kind="ExternalInput").ap()
    c128_d = nc.dram_tensor("c128", [P, 3 * P], BF16, kind="ExternalInput").ap()
    maskb_d = nc.dram_tensor("maskb", [P, 4, 512], BF16, kind="ExternalInput").ap()
    swab_d = nc.dram_tensor("swab", [P, 8, 256], BF16, kind="ExternalInput").ap()
    sink_d = nc.dram_tensor("sink", [64, 4], F32, kind="ExternalInput").ap()
    oT_d = nc.dram_tensor("oT", [512, S_LEN], BF16, kind="ExternalOutput").ap()
    psum = [nc.alloc_psum_tensor(f"ps{i}", [P, 512], F32) for i in range(8)]
    S = Sched(nc)
    dbg_v = nc.dram_tensor("dbg_v", [P, 32, 256], BF16, kind="ExternalOutput").ap() if debug else None
    outs = phase_B(nc, S, HT_d, w_d, c128_d, maskb_d, swab_d, sink_d, oT_d, psum, dbg_v=dbg_v)
    S.run_block(final_waits=outs)
    return nc


class NormCtx:
    def __init__(self, nc, psum_bank):
        A = nc.alloc_sbuf_tensor
        self.sq = [A(f"n_sq{i}", [P, 512], BF16) for i in range(2)]
        self.lnv = A("n_lnv", [P, 512], F32)
        self.rstd = A("n_rstd", [P, 512], F32)
        self.eps = A("n_eps", [P, 1], F32)
        self.ones = {}
        self.ones_tok = {}
        self.ps = psum_bank
        self.sqB = [Buf(), Buf()]
        self.psB = Buf()
        self.lnB = Buf()
        self.rsB = Buf()
        self.nc = nc
        self.eps_tok = None
        self.i = 0

    def prep(self, S, nfeat):
        if self.eps_tok is None:
            self.eps_tok = S.op("dve", lambda e: e.memset(self.eps[:], EPS))
        if nfeat not in self.ones:
            t = self.nc.alloc_sbuf_tensor(f"n_ones{nfeat}", [P, P], BF16)
            self.ones[nfeat] = t
            self.ones_tok[nfeat] = S.op("dve", lambda e: e.memset(t[:], 1.0 / nfeat))


def emit_rmsnorm(S, ctx, src, srcB, dst, dstB, kcs, gain, gcol0, nfeat, T=T_OWN, f32_out=None, gain_tok=None):
    ctx.prep(S, nfeat)
    ones = ctx.ones[nfeat]
    extra = []
    for tc in range(T // 512):
        tsl = slice(tc * 512, (tc + 1) * 512)
        mm = None
        for j, kc in enumerate(kcs):
            b = ctx.i % 2
            ctx.i += 1
            s = S.op("act", lambda e, kc=kc, b=b, tsl=tsl: e.activation(out=ctx.sq[b][:], in_=src[:, kc, tsl], func=AF.Square),
                     deps=srcB[(kc, tc)].rdeps() + ctx.sqB[b].wdeps())
            srcB[(kc, tc)].read(s)
            ctx.sqB[b].wrote(s)
            mm = S.op("pe", lambda e, b=b, j=j: e.matmul(ctx.ps[:], lhsT=ones[:], rhs=ctx.sq[b][:], start=(j == 0), stop=(j == len(kcs) - 1)),
                      deps=ctx.sqB[b].rdeps() + [ctx.ones_tok[nfeat]] + (ctx.psB.wdeps() if j == 0 else []))
            ctx.sqB[b].read(mm)
        ctx.psB.wrote(mm)
        l1 = S.op("act", lambda e: e.activation(out=ctx.lnv[:], in_=ctx.ps[:], func=AF.Ln, bias=ctx.eps[:, 0:1]),
                  deps=ctx.psB.rdeps() + [ctx.eps_tok] + ctx.lnB.wdeps())
        ctx.psB.read(l1)
        ctx.lnB.wrote(l1)
        l2 = S.op("act", lambda e: e.activation(out=ctx.rstd[:], in_=ctx.lnv[:], func=AF.Exp, scale=-0.5),
                  deps=ctx.lnB.rdeps() + ctx.rsB.wdeps())
        ctx.lnB.read(l2)
        ctx.rsB.wrote(l2)
        for j, kc in enumerate(kcs):
            hd = S.op("dve", lambda e, kc=kc, j=j, tsl=tsl: e.scalar_tensor_tensor(
                out=dst[:, kc, tsl], in0=src[:, kc, tsl], scalar=gain[:, gcol0 + j:gcol0 + j + 1], in1=ctx.rstd[:],
                op0=ALU.mult, op1=ALU.mult),
                deps=ctx.rsB.rdeps() + srcB[(kc, tc)].rdeps() + dstB[(kc, tc)].wdeps() + [gain_tok])
            ctx.rsB.read(hd)
            if dst is src:
                dstB[(kc, tc)].wrote(hd)
            else:
                srcB[(kc, tc)].read(hd)
                dstB[(kc, tc)].wrote(hd)
            if f32_out is not None:
                extra += f32_out(kc, j, tc, tsl, ctx.rsB)
    return extra


def bufgrid(n, T=T_OWN):
    return {(kc, tc): Buf() for kc in range(n) for tc in range(T // 512)}


def phase_C(nc, S, xT, xB, mixedT_d, gC_d, w_out_d, w_gu_d, w_down_d, psum, ctx, hT_out_d, yT_out_d, mixed_ready=None):
    A = nc.alloc_sbuf_tensor
    mT = A("c_mT", [P, KC, T_OWN], BF16)
    gC = A("c_gC", [P, 48], F32)
    wbuf = [A(f"c_w{i}", [P, 8192], BF16) for i in range(2)]
    act = A("c_act", [P, 22, T_OWN], BF16)
    sg = [A(f"c_sg{i}", [P, 512], F32) for i in range(2)]
    ystage = [A(f"c_ys{i}", [P, 512], F32) for i in range(2)]
    mB = bufgrid(KC)
    wB = [Buf(), Buf()]
    actB = {(j, tc): Buf() for j in range(22) for tc in range(2)}
    sgB = [Buf(), Buf()]
    ysB = [Buf(), Buf()]
    psB = [Buf() for _ in range(7)]
    wi = [0]
    pidx = [0]

    def w3(b, n):
        return wbuf[b][:, :].rearrange("p (k n) -> p k n", n=n)

    ld_g = S.dma("sp", lambda e: e.dma_start(out=gC[:], in_=gC_d), "cg")
    for r in range(4):
        for part in range(2):
            c0 = 8 * part + 2 * r
            srcv = mixedT_d[r, 256 * part:256 * part + 256, :].rearrange("(c p) t -> p c t", p=P)
            ld = S.dma("sp", lambda e, c0=c0, srcv=srcv: e.dma_start(out=mT[:, c0:c0 + 2, :], in_=srcv), "cm",
                       deps=([mixed_ready] if mixed_ready is not None else []))
    for k in mB:
        mB[k].wrote(ld)

    emit_rmsnorm(S, ctx, mT, mB, mT, mB, list(range(0, 8)), gC, 0, 1024, gain_tok=ld_g)
    emit_rmsnorm(S, ctx, mT, mB, mT, mB, list(range(8, 16)), gC, 8, 1024, gain_tok=ld_g)
    for k in mB:
        pass

    def next_ps():
        pb = pidx[0] % 6
        pidx[0] += 1
        return pb

    def resid_add(pb, cb, tc, tsl):
        ev = S.op("dve", lambda e: e.tensor_tensor(out=xT[:, cb, tsl], in0=xT[:, cb, tsl], in1=psum[pb][:], op=ALU.add),
                  deps=psB[pb].rdeps() + xB[(cb, tc)].wdeps())
        psB[pb].read(ev)
        xB[(cb, tc)].wrote(ev)
        return ev

    for grp in range(4):
        b = wi[0] % 2
        wi[0] += 1
        srcv = w_out_d[:, grp * 512:(grp + 1) * 512].rearrange("(kc p) n -> p kc n", p=P)
        ld = S.dma("pool", lambda e, b=b, srcv=srcv: e.dma_start(out=w3(b, 512), in_=srcv), f"cw{b}", deps=wB[b].wdeps())
        wB[b].wrote(ld)
        for cbl in range(4):
            cb = grp * 4 + cbl
            for tc in range(2):
                tsl = slice(tc * 512, (tc + 1) * 512)
                pb = next_ps()
                for kc in range(KC):
                    mm = S.op("pe", lambda e, b=b, kc=kc, cbl=cbl, tsl=tsl, pb=pb: e.matmul(
                        psum[pb][:], lhsT=w3(b, 512)[:, kc, cbl * P:(cbl + 1) * P], rhs=mT[:, kc, tsl], start=(kc == 0), stop=(kc == KC - 1)),
                        deps=mB[(kc, tc)].rdeps() + (psB[pb].wdeps() + wB[b].rdeps() if kc == 0 else []))
                    mB[(kc, tc)].read(mm)
                psB[pb].wrote(mm)
                wB[b].read(mm)
                resid_add(pb, cb, tc, tsl)

    emit_rmsnorm(S, ctx, xT, xB, mT, mB, list(range(KC)), gC, 16, D)

    for half in range(2):
        for grp in range(11):
            f0 = half * 22 + grp * 2
            b = wi[0] % 2
            wi[0] += 1
            sg_ = w_gu_d[:, f0 * P:f0 * P + 256].rearrange("(kc p) n -> p kc n", p=P)
            su_ = w_gu_d[:, DFF + f0 * P:DFF + f0 * P + 256].rearrange("(kc p) n -> p kc n", p=P)
            ld1 = S.dma("pool", lambda e, b=b, sg_=sg_: e.dma_start(out=w3(b, 512)[:, :, 0:256], in_=sg_), f"cw{b}", deps=wB[b].wdeps())
            ld = S.dma("pool", lambda e, b=b, su_=su_: e.dma_start(out=w3(b, 512)[:, :, 256:512], in_=su_), f"cw{b}", deps=wB[b].wdeps())
            wB[b].wrote(ld)
            for fl in range(2):
                j = grp * 2 + fl
                for tc in range(2):
                    tsl = slice(tc * 512, (tc + 1) * 512)
                    pg = next_ps()
                    pu = next_ps()
                    for (pb, coff) in ((pg, 0), (pu, 256)):
                        for kc in range(KC):
                            mm = S.op("pe", lambda e, b=b, kc=kc, fl=fl, tsl=tsl, pb=pb, coff=coff: e.matmul(
                                psum[pb][:], lhsT=w3(b, 512)[:, kc, coff + fl * P:coff + (fl + 1) * P], rhs=mT[:, kc, tsl],
                                start=(kc == 0), stop=(kc == KC - 1)),
                                deps=mB[(kc, tc)].rdeps() + (psB[pb].wdeps() + wB[b].rdeps() if kc == 0 else []))
                            mB[(kc, tc)].read(mm)
                        psB[pb].wrote(mm)
                        wB[b].read(mm)
                    sb_ = (j * 2 + tc) % 2
                    a1 = S.op("act", lambda e, pg=pg, sb_=sb_: e.activation(out=sg[sb_][:], in_=psum[pg][:], func=AF.Silu),
                              deps=psB[pg].rdeps() + sgB[sb_].wdeps())
                    psB[pg].read(a1)
                    sgB[sb_].wrote(a1)
                    a2 = S.op("dve", lambda e, pu=pu, sb_=sb_, j=j, tsl=tsl: e.tensor_tensor(
                        out=act[:, j, tsl], in0=sg[sb_][:], in1=psum[pu][:], op=ALU.mult),
                        deps=psB[pu].rdeps() + sgB[sb_].rdeps() + actB[(j, tc)].wdeps())
                    psB[pu].read(a2)
                    sgB[sb_].read(a2)
                    actB[(j, tc)].wrote(a2)
        for grp in range(8):
            b = wi[0] % 2
            wi[0] += 1
            srcv = w_down_d[half * 22 * P:(half + 1) * 22 * P, grp * 256:(grp + 1) * 256].rearrange("(j p) n -> p j n", p=P)
            ld = S.dma("pool", lambda e, b=b, srcv=srcv: e.dma_start(out=w3(b, 256)[:, 0:22, :], in_=srcv), f"cw{b}", deps=wB[b].wdeps())
            wB[b].wrote(ld)
            for cbl in range(2):
                cb = grp * 2 + cbl
                for tc in range(2):
                    tsl = slice(tc * 512, (tc + 1) * 512)
                    pb = next_ps()
                    for j in range(22):
                        mm = S.op("pe", lambda e, b=b, j=j, cbl=cbl, tsl=tsl, pb=pb: e.matmul(
                            psum[pb][:], lhsT=w3(b, 256)[:, j, cbl * P:(cbl + 1) * P], rhs=act[:, j, tsl], start=(j == 0), stop=(j == 21)),
                            deps=actB[(j, tc)].rdeps() + (psB[pb].wdeps() + wB[b].rdeps() if j == 0 else []))
                        actB[(j, tc)].read(mm)
                    psB[pb].wrote(mm)
                    wB[b].read(mm)
                    resid_add(pb, cb, tc, tsl)

    y_dmas = []

    def f32_out(kc, j, tc, tsl, rsB):
        yb = (kc + tc) % 2
        t = S.op("dve", lambda e: e.scalar_tensor_tensor(out=ystage[yb][:], in0=xT[:, kc, tsl], scalar=gC[:, 32 + j:33 + j], in1=ctx.rstd[:],
                                                         op0=ALU.mult, op1=ALU.mult), deps=rsB.rdeps() + ysB[yb].wdeps())
        rsB.read(t)
        ysB[yb].wrote(t)
        st = S.dma("sp", lambda e: e.dma_start(out=yT_out_d[kc * P:(kc + 1) * P, tsl], in_=ystage[yb][:]), f"cy{yb}", deps=[t])
        ysB[yb].read(st)
        y_dmas.append(st)
        return [st]

    emit_rmsnorm(S, ctx, xT, xB, mT, mB, list(range(KC)), gC, 32, D, f32_out=f32_out if yT_out_d is not None else None)
    outs = list(y_dmas)
    if hT_out_d is not None:
        st = S.dma("sp", lambda e: e.dma_start(out=hT_out_d.rearrange("(kc p) t -> p kc t", p=P), in_=mT[:]), "ch",
                   deps=[mB[k].w for k in mB])
        outs.append(st)
    return outs


def build_C():
    nc = bass.Bass("TRN2", target_bir_lowering=False)
    xT_d = nc.dram_tensor("xT", [D, T_OWN], F32, kind="ExternalInput").ap()
    mixedT_d = nc.dram_tensor("mixedT", [4, 512, T_OWN], BF16, kind="ExternalInput").ap()
    gC_d = nc.dram_tensor("gC", [P, 48], F32, kind="ExternalInput").ap()
    w_out_d = nc.dram_tensor("w_out", [D, D], F32, kind="ExternalInput").ap()
    w_gu_d = nc.dram_tensor("w_gu", [D, 2 * DFF], F32, kind="ExternalInput").ap()
    w_down_d = nc.dram_tensor("w_down", [DFF, D], F32, kind="ExternalInput").ap()
    xT_o = nc.dram_tensor("xT_out", [D, T_OWN], F32, kind="ExternalOutput").ap()
    hT_o = nc.dram_tensor("hT_out", [D, T_OWN], BF16, kind="ExternalOutput").ap()
    yT_o = nc.dram_tensor("yT_out", [D, T_OWN], F32, kind="ExternalOutput").ap()
    psum = [nc.alloc_psum_tensor(f"ps{i}", [P, 512], F32) for i in range(8)]
    S = Sched(nc)
    xT = nc.alloc_sbuf_tensor("xT_sb", [P, KC, T_OWN], F32)
    xB = bufgrid(KC)
    ld = S.dma("sp", lambda e: e.dma_start(out=xT[:], in_=xT_d.rearrange("(kc p) t -> p kc t", p=P)), "cx")
    for k in xB:
        xB[k].wrote(ld)
    ctx = NormCtx(nc, psum[7])
    outs = phase_C(nc, S, xT, xB, mixedT_d, gC_d, w_out_d, w_gu_d, w_down_d, psum, ctx, hT_o, yT_o)
    st = S.dma("sp", lambda e: e.dma_start(out=xT_o.rearrange("(kc p) t -> p kc t", p=P), in_=xT[:]), "cxo",
               deps=[xB[k].w for k in xB])
    S.run_block(final_waits=outs + [st])
    return nc


def build_A():
    nc = bass.Bass("TRN2", target_bir_lowering=False)
    xT_d = nc.dram_tensor("xT", [D, T_OWN], F32, kind="ExternalInput").ap()
    g_d = nc.dram_tensor("gA", [P, KC], F32, kind="ExternalInput").ap()
    hT_o = nc.dram_tensor("hT_out", [D, T_OWN], BF16, kind="ExternalOutput").ap()
    psum = [nc.alloc_psum_tensor(f"ps{i}", [P, 512], F32) for i in range(8)]
    S = Sched(nc)
    xT = nc.alloc_sbuf_tensor("xT_sb", [P, KC, T_OWN], F32)
    hT = nc.alloc_sbuf_tensor("hT_sb", [P, KC, T_OWN], BF16)
    gA = nc.alloc_sbuf_tensor("gA_sb", [P, KC], F32)
    xB, hB = bufgrid(KC), bufgrid(KC)
    ld = S.dma("sp", lambda e: e.dma_start(out=xT[:], in_=xT_d.rearrange("(kc p) t -> p kc t", p=P)), "ax")
    ld_g = S.dma("sp", lambda e: e.dma_start(out=gA[:], in_=g_d), "ag")
    for k in xB:
        xB[k].wrote(ld)
    ctx = NormCtx(nc, psum[7])
    emit_rmsnorm(S, ctx, xT, xB, hT, hB, list(range(KC)), gA, 0, D, gain_tok=ld_g)
    st = S.dma("sp", lambda e: e.dma_start(out=hT_o.rearrange("(kc p) t -> p kc t", p=P), in_=hT[:]), "ah",
               deps=[hB[k].w for k in hB])
    S.run_block(final_waits=[st])
    return nc


def _core_w_in(w_in_l, g):
    sbq = w_in_l[:, 0:1024][:, 256 * g:256 * g + 256]
    sbk = w_in_l[:, 1024:2048][:, 256 * g:256 * g + 256]
    sbv = w_in_l[:, 2048:3072][:, 256 * g:256 * g + 256]
    swq = w_in_l[:, 3072:4096][:, 256 * g:256 * g + 256]
    swk = w_in_l[:, 4096:4352][:, 64 * g:64 * g + 64]
    swv = w_in_l[:, 4352:4608][:, 64 * g:64 * g + 64]
    return np.ascontiguousarray(np.concatenate([sbq, sbk, swq, swk, swk, sbv, swv], axis=1))


def _pk(g):
    return np.ascontiguousarray(np.asarray(g, np.float32).reshape(-1, P).T)


def kernel(x, ln_mix, w_in, sb_out_norm, swa_out_norm, swa_sinks, w_out, ln_ffn, w_gate_up, w_down, ln_final):
    x = np.asarray(x, np.float32)
    ln_mix, w_in, w_out = np.asarray(ln_mix, np.float32), np.asarray(w_in, np.float32), np.asarray(w_out, np.float32)
    sb_out_norm, swa_out_norm = np.asarray(sb_out_norm, np.float32), np.asarray(swa_out_norm, np.float32)
    swa_sinks, ln_ffn, ln_final = np.asarray(swa_sinks, np.float32), np.asarray(ln_ffn, np.float32), np.asarray(ln_final, np.float32)
    w_gate_up, w_down = np.asarray(w_gate_up, np.float32), np.asarray(w_down, np.float32)
    depth = w_in.shape[0]
    cores = list(range(NCORES))
    c128, maskb = _const_tables()
    swab = [_swa_tables(g) for g in range(4)]

    ncA = build_A()
    ims = []
    for c in cores:
        b, r = c // 4, c % 4
        ims.append({"xT": np.ascontiguousarray(x[b, T_OWN * r:T_OWN * (r + 1), :].T), "gA": _pk(ln_mix[0])})
    res = run_bass_kernel_spmd(ncA, ims, core_ids=cores)
    xT = [im["xT"] for im in ims]
    hT = [res.results[c]["hT_out"] for c in cores]
    yT = None
    for l in range(depth):
        ncB = build_B()
        ims = []
        for c in cores:
            b, g = c // 4, c % 4
            HT = np.ascontiguousarray(np.stack([hT[4 * b + r] for r in range(4)]))
            ims.append({"HT": HT, "w_in": _core_w_in(w_in[l], g), "c128": c128, "maskb": maskb, "swab": swab[g],
                        "sink": np.ascontiguousarray(np.broadcast_to(swa_sinks[l, 4 * g:4 * g + 4], (64, 4)))})
        res = run_bass_kernel_spmd(ncB, ims, core_ids=cores)
        oT = [res.results[c]["oT"] for c in cores]
        ncC = build_C()
        g_next = ln_mix[l + 1] if l + 1 < depth else ln_final
        gC = np.ascontiguousarray(np.concatenate([_pk(sb_out_norm[l]), _pk(swa_out_norm[l]), _pk(ln_ffn[l]), _pk(g_next)], axis=1))
        ims = []
        for c in cores:
            b, r = c // 4, c % 4
            mixedT = np.ascontiguousarray(np.stack([oT[4 * b + g][:, T_OWN * r:T_OWN * (r + 1)] for g in range(4)]))
            ims.append({"xT": xT[c], "mixedT": mixedT, "gC": gC, "w_out": w_out[l], "w_gu": w_gate_up[l], "w_down": w_down[l]})
        res = run_bass_kernel_spmd(ncC, ims, core_ids=cores)
        xT = [res.results[c]["xT_out"] for c in cores]
        hT = [res.results[c]["hT_out"] for c in cores]
        yT = [res.results[c]["yT_out"] for c in cores]
    out = np.empty((2, S_LEN, D), np.float32)
    for c in cores:
        b, r = c // 4, c % 4
        out[b, T_OWN * r:T_OWN * (r + 1), :] = yT[c].T
    return out
```
